# Optimizing a Trainium2 kernel written in Bass

```python
import jax, jax.numpy as jnp
from jax import lax
import numpy as np

D_MODEL = 2048
BATCH = 4
SEQ = 4096
DEPTH = 2

GRID_W = 64
CTX_LEN = 256
EPS = 1e-6

MLA_HEADS = 8
MLA_NOPE = 128
MLA_ROPE = 64
MLA_V = 128
MLA_Q_RANK = 512
MLA_KV_RANK = 512
MLA_SCALE = (MLA_NOPE + MLA_ROPE) ** -0.5
ROPE_BASE = 10000.0

NA_HEADS = 4
NA_HEAD_DIM = 128
NA_KH = 8
NA_KW = 16
NA_SCALE = NA_HEAD_DIM ** -0.5

FN_GROUPS = 4
FN_CH = 128

MLA_WIDTH = MLA_HEADS * MLA_V
NA_WIDTH = NA_HEADS * NA_HEAD_DIM
FN_WIDTH = FN_GROUPS * FN_CH
MIX_WIDTH = MLA_WIDTH + NA_WIDTH + FN_WIDTH
IN_SPLITS = (MLA_Q_RANK, MLA_KV_RANK, MLA_ROPE, NA_WIDTH, NA_WIDTH, NA_WIDTH, FN_WIDTH)
IN_COLS = sum(IN_SPLITS)

D_FF = 5632
CONV_W = 3
Q_BLOCK = 128

kernel_name = 'hybrid_mla_natten_fnet_convffn_prefix'


def rmsnorm(x, g):
    xf = x.astype(jnp.float32)
    y = xf * lax.rsqrt(jnp.mean(xf * xf, axis=-1, keepdims=True) + EPS)
    return (y * g.astype(jnp.float32)).astype(x.dtype)


def split_columns(u):
    offs = np.cumsum(IN_SPLITS)[:-1].tolist()
    return jnp.split(u, offs, axis=-1)


def axial_angles(rows, cols):
    n = MLA_ROPE // 4
    freqs = ROPE_BASE ** (-jnp.arange(n, dtype=jnp.float32) / n)
    ang_r = rows.astype(jnp.float32)[:, None, None] * freqs
    ang_c = cols.astype(jnp.float32)[:, None, None] * freqs
    return ang_r, ang_c


def rotate(x, ang):
    n = x.shape[-1] // 2
    x1, x2 = x[..., :n], x[..., n:]
    cos, sin = jnp.cos(ang).astype(x.dtype), jnp.sin(ang).astype(x.dtype)
    return jnp.concatenate([x1 * cos - x2 * sin, x2 * cos + x1 * sin], axis=-1)


def axial_rope(x, ang_r, ang_c):
    h = x.shape[-1] // 2
    return jnp.concatenate([rotate(x[..., :h], ang_r), rotate(x[..., h:], ang_c)], axis=-1)


def mla_qkv(c_q, c_kv, k_r, g_q, w_uq, g_kv, w_ukv, angles):
    B, L, _ = c_q.shape
    q = (rmsnorm(c_q, g_q) @ w_uq).reshape(B, L, MLA_HEADS, MLA_NOPE + MLA_ROPE)
    kv = (rmsnorm(c_kv, g_kv) @ w_ukv).reshape(B, L, MLA_HEADS, MLA_NOPE + MLA_V)
    q_nope, q_rope = q[..., :MLA_NOPE], q[..., MLA_NOPE:]
    k_nope, v = kv[..., :MLA_NOPE], kv[..., MLA_NOPE:]
    k_rope = k_r[:, :, None, :]
    if angles is not None:
        q_rope = axial_rope(q_rope, *angles)
        k_rope = axial_rope(k_rope, *angles)
    q = jnp.concatenate([q_nope, q_rope], axis=-1)
    k = jnp.concatenate([k_nope, jnp.broadcast_to(k_rope, (B, L, MLA_HEADS, MLA_ROPE))], axis=-1)
    return q, k, v


def blocked_attention(q, k, v, scale):
    B, L, H, dq = q.shape
    dv = v.shape[-1]
    nb = L // Q_BLOCK
    qb = q.reshape(B, nb, Q_BLOCK, H, dq).transpose(1, 0, 2, 3, 4)

    def one_block(qi):
        s = jnp.einsum('bqhd,bkhd->bhqk', qi, k, preferred_element_type=jnp.float32) * scale
        p = jax.nn.softmax(s, axis=-1).astype(v.dtype)
        return jnp.einsum('bhqk,bkhd->bqhd', p, v)

    o = lax.map(one_block, qb)
    return o.transpose(1, 0, 2, 3, 4).reshape(B, L, H * dv)


def neighborhood_attention(q, k, v, k_ctx, v_ctx, rpb):
    B, S, H, d = q.shape
    rows = S // GRID_W
    kh = min(NA_KH, rows)
    qg = q.reshape(B, rows, GRID_W, H, d)
    kg = k.reshape(B, rows, GRID_W, H, d)
    vg = v.reshape(B, rows, GRID_W, H, d)
    w_idx = jnp.arange(GRID_W)
    col_start = jnp.clip(w_idx - NA_KW // 2, 0, GRID_W - NA_KW)
    col_idx = col_start[:, None] + jnp.arange(NA_KW)
    rel_col = col_idx - w_idx[:, None] + (NA_KW - 1)
    nwin = kh * NA_KW

    def one_row(r):
        rs = jnp.clip(r - kh // 2, 0, rows - kh)
        q_r = lax.dynamic_index_in_dim(qg, r, axis=1, keepdims=False)
        k_rows = lax.dynamic_slice_in_dim(kg, rs, kh, axis=1)
        v_rows = lax.dynamic_slice_in_dim(vg, rs, kh, axis=1)
        k_win = k_rows[:, :, col_idx]
        v_win = v_rows[:, :, col_idx]
        rel_row = rs + jnp.arange(kh) - r + (NA_KH - 1)
        bias = rpb[:, rel_row[None, :, None], rel_col[:, None, :]]
        s_win = jnp.einsum('bwhd,bawjhd->bhwaj', q_r, k_win, preferred_element_type=jnp.float32) * NA_SCALE
        s_win = s_win + bias.astype(jnp.float32)
        s_ctx = jnp.einsum('bwhd,bchd->bhwc', q_r, k_ctx, preferred_element_type=jnp.float32) * NA_SCALE
        s = jnp.concatenate([s_win.reshape(B, H, GRID_W, nwin), s_ctx], axis=-1)
        p = jax.nn.softmax(s, axis=-1).astype(v.dtype)
        p_win = p[..., :nwin].reshape(B, H, GRID_W, kh, NA_KW)
        p_ctx = p[..., nwin:]
        return (jnp.einsum('bhwaj,bawjhd->bwhd', p_win, v_win)
                + jnp.einsum('bhwc,bchd->bwhd', p_ctx, v_ctx))

    o = lax.map(one_row, jnp.arange(rows))
    return o.transpose(1, 0, 2, 3, 4).reshape(B, S, H * d)


def fourier_mix(f, w_fnet):
    B, L, _ = f.shape
    fg = f.reshape(B, L, FN_GROUPS, FN_CH).astype(jnp.float32)
    spec = jnp.fft.fft2(fg, axes=(1, 3), norm='ortho').real.astype(f.dtype)
    return jnp.einsum('blgc,gcd->blgd', spec, w_fnet).reshape(B, L, FN_WIDTH)


def conv_ffn(h, w_up, conv_w, conv_b, w_down):
    L = h.shape[1]
    u = h @ w_up
    up = jnp.pad(u, ((0, 0), (CONV_W // 2, CONV_W // 2), (0, 0)))
    u = sum(up[:, j:j + L] * conv_w[j] for j in range(CONV_W)) + conv_b
    gate, val = jnp.split(u, 2, axis=-1)
    return (jax.nn.silu(gate) * val) @ w_down


def heads(t, n_heads):
    B, L, _ = t.shape
    return t.reshape(B, L, n_heads, -1)


def setup_inputs(seed: int = 0) -> dict:
    key = jax.random.key(seed)
    ks = jax.random.split(key, 24)
    D = D_MODEL

    def nrm(k, shape, scale):
        return jax.random.normal(k, shape, jnp.float32) * scale

    return {
        'x': nrm(ks[0], (BATCH, SEQ, D), 1.0),
        'c': nrm(ks[1], (BATCH, D), 1.0),
        'ctx': nrm(ks[2], (BATCH, CTX_LEN, D), 1.0),
        'c_ctx': nrm(ks[3], (D,), 1.0),
        'w_mod': nrm(ks[4], (DEPTH, D, 6 * D), D ** -0.5),
        'b_mod': nrm(ks[5], (DEPTH, 6 * D), 0.02),
        'g_attn': 1.0 + nrm(ks[6], (DEPTH, D), 0.02),
        'g_ffn': 1.0 + nrm(ks[7], (DEPTH, D), 0.02),
        'w_in': nrm(ks[8], (DEPTH, D, IN_COLS), D ** -0.5),
        'g_q': 1.0 + nrm(ks[9], (DEPTH, MLA_Q_RANK), 0.02),
        'w_uq': nrm(ks[10], (DEPTH, MLA_Q_RANK, MLA_HEADS * (MLA_NOPE + MLA_ROPE)), MLA_Q_RANK ** -0.5),
        'g_kv': 1.0 + nrm(ks[11], (DEPTH, MLA_KV_RANK), 0.02),
        'w_ukv': nrm(ks[12], (DEPTH, MLA_KV_RANK, MLA_HEADS * (MLA_NOPE + MLA_V)), MLA_KV_RANK ** -0.5),
        'na_rpb': nrm(ks[13], (DEPTH, NA_HEADS, 2 * NA_KH - 1, 2 * NA_KW - 1), 0.1),
        'w_fnet': nrm(ks[14], (DEPTH, FN_GROUPS, FN_CH, FN_CH), FN_CH ** -0.5),
        'w_out': nrm(ks[15], (DEPTH, MIX_WIDTH, D), MIX_WIDTH ** -0.5),
        'w_up': nrm(ks[16], (DEPTH, D, 2 * D_FF), D ** -0.5),
        'conv_w': nrm(ks[17], (DEPTH, CONV_W, 2 * D_FF), CONV_W ** -0.5),
        'conv_b': nrm(ks[18], (DEPTH, 2 * D_FF), 0.02),
        'w_down': nrm(ks[19], (DEPTH, D_FF, D), D_FF ** -0.5),
        'g_final': 1.0 + nrm(ks[20], (D,), 0.02),
    }


def reference(x, c, ctx, c_ctx, w_mod, b_mod, g_attn, g_ffn, w_in, g_q, w_uq, g_kv, w_ukv,
              na_rpb, w_fnet, w_out, w_up, conv_w, conv_b, w_down, g_final):
    S = x.shape[1]
    pos = jnp.arange(S)
    angles = axial_angles(pos // GRID_W, pos % GRID_W)
    xc = ctx
    mod_in = jax.nn.silu(c)
    modc_in = jax.nn.silu(c_ctx)

    for l in range(DEPTH):
        ctx_out = l < DEPTH - 1
        sh1, sc1, gt1, sh2, sc2, gt2 = jnp.split((mod_in @ w_mod[l] + b_mod[l])[:, None, :], 6, axis=-1)
        csh1, csc1, cgt1, csh2, csc2, cgt2 = jnp.split(modc_in @ w_mod[l] + b_mod[l], 6, axis=-1)

        h = rmsnorm(x, g_attn[l]) * (1 + sc1) + sh1
        hc = rmsnorm(xc, g_attn[l]) * (1 + csc1) + csh1
        cq, ckv, kr, qn, kn, vn, f = split_columns(h @ w_in[l])
        cq_c, ckv_c, kr_c, qn_c, kn_c, vn_c, f_c = split_columns(hc @ w_in[l])

        q_m, k_m, v_m = mla_qkv(cq, ckv, kr, g_q[l], w_uq[l], g_kv[l], w_ukv[l], angles)
        q_mc, k_mc, v_mc = mla_qkv(cq_c, ckv_c, kr_c, g_q[l], w_uq[l], g_kv[l], w_ukv[l], None)
        o_mla = blocked_attention(q_m, jnp.concatenate([k_m, k_mc], axis=1),
                                  jnp.concatenate([v_m, v_mc], axis=1), MLA_SCALE)
        kn_c, vn_c = heads(kn_c, NA_HEADS), heads(vn_c, NA_HEADS)
        o_na = neighborhood_attention(heads(qn, NA_HEADS), heads(kn, NA_HEADS), heads(vn, NA_HEADS),
                                      kn_c, vn_c, na_rpb[l])
        o_fn = fourier_mix(f, w_fnet[l])
        x = x + gt1 * (jnp.concatenate([o_mla, o_na, o_fn], axis=-1) @ w_out[l])

        if ctx_out:
            o_mla_c = blocked_attention(q_mc, k_mc, v_mc, MLA_SCALE)
            o_na_c = blocked_attention(heads(qn_c, NA_HEADS), kn_c, vn_c, NA_SCALE)
            o_fn_c = fourier_mix(f_c, w_fnet[l])
            xc = xc + cgt1 * (jnp.concatenate([o_mla_c, o_na_c, o_fn_c], axis=-1) @ w_out[l])

        h2 = rmsnorm(x, g_ffn[l]) * (1 + sc2) + sh2
        x = x + gt2 * conv_ffn(h2, w_up[l], conv_w[l], conv_b[l], w_down[l])
        if ctx_out:
            h2c = rmsnorm(xc, g_ffn[l]) * (1 + csc2) + csh2
            xc = xc + cgt2 * conv_ffn(h2c, w_up[l], conv_w[l], conv_b[l], w_down[l])

    return rmsnorm(x, g_final)
```

```python
import contextlib
import numpy as np
import ml_dtypes
import concourse.bass as bass
import concourse.mybir as mybir
from concourse.bass_utils import run_bass_kernel_spmd

F32, BF16 = mybir.dt.float32, mybir.dt.bfloat16
AF = mybir.ActivationFunctionType
ALU = mybir.AluOpType

D = 2048
SEQ = 4096
CTX = 256
NTOK = SEQ + CTX
NT = NTOK // 128
DEPTH = 2
DFF = 5632
EPS = 1e-6
MLA_SCALE = 192.0 ** -0.5
NA_SCALE = 128.0 ** -0.5
NEG = -30000.0
W_IN_BLOCKS = [(0, 512), (512, 512), (1024, 64), (1088, 512), (1600, 512), (2112, 512), (2624, 512)]
NCORES = 4


class Buf:
    __slots__ = ("w", "r", "track")

    def __init__(self, track=True):
        self.w = None
        self.r = {}
        self.track = track


class V:
    def __init__(self, ap, bufs):
        self.ap = ap
        self.bufs = list(bufs)


class Tl:
    def __init__(self, t, track=True):
        self.t = t
        self.bufs = {}
        self.track = track

    def b(self, *keys):
        return [self.bufs.setdefault(k, Buf(self.track)) for k in keys]

    def v(self, idx=None, keys=(0,)):
        ap = self.t[idx] if idx is not None else self.t[:]
        return V(ap, self.b(*keys))


class Eng:
    def __init__(self, kb, name, eng):
        self.name = name
        self.eng = eng
        self.sem = kb.newsem(name)
        self.n = 0
        self.waited = {}
        self.pending = False


class KB:
    def __init__(self, nc, es):
        self.nc = nc
        self.es = es
        self.semcount = 0
        self.E = {}
        for n, e in (("pe", nc.tensor), ("act", nc.scalar), ("dve", nc.vector), ("pool", nc.gpsimd), ("sp", nc.sync)):
            self.E[n] = Eng(self, n, e)
        self.dmasems = {q: [[self.newsem("d" + q), 0] for _ in range(12)] for q in ("sp", "pool", "act")}
        self.dmarr = {"sp": 0, "pool": 0, "act": 0}
        self.pstack = None
        self.uid = 0

    def newsem(self, name):
        self.semcount += 1
        return self.es.enter_context(self.nc.semaphore(f"{name}_{self.semcount}"))

    def phase_begin(self):
        self.pstack = contextlib.ExitStack()

    def phase_end(self):
        self.barrier()
        self.pstack.close()
        self.pstack = None

    def sb(self, name, shape, dt):
        self.uid += 1
        t = self.pstack.enter_context(self.nc.sbuf_tensor(f"{name}_{self.uid}", shape, dt))
        return Tl(t)

    def _wait(self, E, sem, val):
        k = id(sem)
        if E.waited.get(k, 0) >= val:
            return
        E.eng.wait_ge(sem, val)
        E.waited[k] = val

    def _waits(self, E, reads, writes):
        need = {}

        def add(ev):
            if ev is None:
                return
            sem, val = ev
            k = id(sem)
            if k not in need or need[k][1] < val:
                need[k] = (sem, val)

        for b in reads:
            if b.track:
                add(b.w)
        for b in writes:
            if b.track:
                add(b.w)
                for ev in b.r.values():
                    add(ev)
        for sem, val in need.values():
            if sem is E.sem and E.name == "pe":
                continue
            self._wait(E, sem, val)

    def _done(self, ev, reads, writes):
        for b in writes:
            if b.track:
                b.w = ev
                b.r = {}
        for b in reads:
            if b.track:
                b.r[id(ev[0])] = ev

    def issue(self, en, fn, reads, writes, sig=True):
        E = self.E[en]
        self._waits(E, reads, writes)
        if E.n >= 30000 and not E.pending:
            E.sem = self.newsem(en)
            E.n = 0
        ins = fn(E.eng)
        if sig:
            E.n += 1
            ins.then_inc(E.sem, 1)
            E.pending = False
            self._done((E.sem, E.n), reads, writes)
        else:
            E.pending = True
            self._done((E.sem, E.n + 1), reads, writes)

    def dma(self, q, out, in_, **kw):
        E = self.E[q]
        self._waits(E, in_.bufs, out.bufs)
        pool = self.dmasems[q]
        i = self.dmarr[q]
        self.dmarr[q] = (i + 1) % len(pool)
        ent = pool[i]
        if ent[1] > 0:
            self._wait(E, ent[0], ent[1])
        E.eng.dma_start(out=out.ap, in_=in_.ap, **kw).then_inc(ent[0], 16)
        ent[1] += 16
        self._done((ent[0], ent[1]), in_.bufs, out.bufs)

    def barrier(self):
        evs = [(E.sem, E.n, E.name) for E in self.E.values() if E.n > 0]
        evs += [(e[0], e[1], "dma") for q in self.dmasems.values() for e in q if e[1] > 0]
        for E in self.E.values():
            for sem, val, nm in evs:
                if sem is E.sem and E.name == "pe":
                    continue
                self._wait(E, sem, val)

    def mm(self, out, lhsT, rhs, start=True, stop=True, sig=None):
        if sig is None:
            sig = stop
        self.issue("pe", lambda e: e.matmul(out.ap, lhsT.ap, rhs.ap, start=start, stop=stop),
                   lhsT.bufs + rhs.bufs, out.bufs, sig=sig)

    def tr(self, out, in_, ident):
        self.issue("pe", lambda e: e.transpose(out.ap, in_.ap, ident.ap), in_.bufs + ident.bufs, out.bufs)

    def act(self, out, in_, func, bias=None, scale=None, accum=None):
        kw = {}
        reads = list(in_.bufs)
        writes = list(out.bufs)
        if bias is not None:
            if isinstance(bias, V):
                kw["bias"] = bias.ap
                reads += bias.bufs
            else:
                kw["bias"] = bias
        if scale is not None:
            if isinstance(scale, V):
                kw["scale"] = scale.ap
                reads += scale.bufs
            else:
                kw["scale"] = scale
        if accum is not None:
            kw["accum_out"] = accum.ap
            writes += accum.bufs
        self.issue("act", lambda e: e.activation(out=out.ap, in_=in_.ap, func=func, **kw), reads, writes)

    def tt(self, en, out, in0, in1, op):
        self.issue(en, lambda e: e.tensor_tensor(out=out.ap, in0=in0.ap, in1=in1.ap, op=op),
                   in0.bufs + in1.bufs, out.bufs)

    def ts(self, en, out, in0, s1, s2, op0, op1=None):
        reads = list(in0.bufs)
        a1 = s1
        a2 = s2
        if isinstance(s1, V):
            a1 = s1.ap
            reads += s1.bufs
        if isinstance(s2, V):
            a2 = s2.ap
            reads += s2.bufs
        if op1 is None:
            self.issue(en, lambda e: e.tensor_scalar(out=out.ap, in0=in0.ap, scalar1=a1, scalar2=None, op0=op0),
                       reads, out.bufs)
        else:
            self.issue(en, lambda e: e.tensor_scalar(out=out.ap, in0=in0.ap, scalar1=a1, scalar2=a2, op0=op0, op1=op1),
                       reads, out.bufs)

    def stt(self, out, in0, scalar, in1, op0, op1):
        reads = in0.bufs + in1.bufs
        a = scalar
        if isinstance(scalar, V):
            a = scalar.ap
            reads = reads + scalar.bufs
        self.issue("dve", lambda e: e.scalar_tensor_tensor(out=out.ap, in0=in0.ap, scalar=a, in1=in1.ap, op0=op0, op1=op1),
                   reads, out.bufs)

    def copy(self, en, out, in_):
        if en == "act":
            self.act(out, in_, AF.Copy)
        else:
            self.issue(en, lambda e: e.tensor_copy(out=out.ap, in_=in_.ap), in_.bufs, out.bufs)

    def recip(self, out, in_):
        self.issue("dve", lambda e: e.reciprocal(out=out.ap, in_=in_.ap), in_.bufs, out.bufs)

    def memset(self, en, out, val):
        self.issue(en, lambda e: e.memset(out.ap, val), [], out.bufs)


class Prog:
    def __init__(self, stop_after=None, dump=()):
        self.stop_after = stop_after
        self.dump = set(dump)
        self.nc = bass.Bass("TRN2", target_bir_lowering=False)
        self.dr = {}

    def din(self, name, shape, dt):
        self.dr[name] = Tl(self.nc.dram_tensor(name, list(shape), dt, kind="ExternalInput").ap(), track=False)
        return self.dr[name]

    def dsc(self, name, shape, dt):
        kind = "ExternalOutput" if name in self.dump else "Internal"
        self.dr[name] = Tl(self.nc.dram_tensor(name, list(shape), dt, kind=kind).ap(), track=True)
        return self.dr[name]

    def declare(self):
        di, ds = self.din, self.dsc
        di("xin", [NTOK, D], F32)
        di("ccT", [128, 16, 33], F32)
        di("w_mod", [DEPTH, D, 6 * D], F32)
        di("b_mod", [DEPTH, 6 * D], F32)
        di("g_attn", [DEPTH, D], F32)
        di("g_ffn", [DEPTH, D], F32)
        di("w_in", [DEPTH, D, 3136], F32)
        di("g_q", [DEPTH, 512], F32)
        di("g_kv", [DEPTH, 512], F32)
        di("w_uq", [DEPTH, 512, 1536], F32)
        di("w_uqs", [DEPTH, 512, 512], F32)
        di("w_ukv", [DEPTH, 512, 2048], F32)
        di("w_fnet", [DEPTH, 4, 128, 128], F32)
        di("w_out", [DEPTH, D, D], F32)
        di("w_up", [DEPTH, D, 2 * DFF], F32)
        di("convp", [DEPTH, 128, 88, 4], F32)
        di("w_down", [DEPTH, DFF, D], F32)
        di("g_final", [D], F32)
        di("ident", [128, 128], BF16)
        di("ones", [128, 128], BF16)
        di("ropeTM", [NTOK, 128], F32)
        di("ropeC1", [64, NTOK], F32)
        di("ropeC2", [64, NTOK], F32)
        di("dftC", [8, 4, 128, 8, 512], BF16)
        di("dftS", [8, 4, 128, 8, 512], BF16)
        di("dftCc", [CTX, CTX], BF16)
        di("dftSc", [CTX, CTX], BF16)
        di("cdft", [128, 256], F32)
        di("nab", [DEPTH, 4, 3, 8, 128, 512], F32)
        self.dr["out"] = Tl(self.nc.dram_tensor("out", [SEQ, D], F32, kind="ExternalOutput").ap(), track=True)
        ds("xs", [NTOK, D], F32)
        ds("modv", [DEPTH, 2, 6 * D], F32)
        ds("cqnT", [512, NTOK], BF16)
        ds("ckvnT", [512, NTOK], BF16)
        ds("krT", [64, NTOK], BF16)
        ds("qnT", [512, NTOK], BF16)
        ds("knT", [512, NTOK], BF16)
        ds("vn", [NTOK, 512], BF16)
        ds("xab", [NTOK, 1024], BF16)
        ds("kTn", [8, 128, NTOK], BF16)
        ds("vall", [NTOK, 1024], BF16)
        ds("mixT", [D, NTOK], BF16)
        ds("h2T", [D, NTOK], BF16)
        ds("aT", [DFF, NTOK], BF16)

    def build(self):
        nc = self.nc
        self.declare()
        with contextlib.ExitStack() as es:
            k = KB(nc, es)
            self.k = k
            self.ps = [Tl(es.enter_context(nc.psum_tensor(f"ps{i}", [128, 512], F32))) for i in range(6)]
            self.pb = [Tl(es.enter_context(nc.psum_tensor(f"pb{i}", [128, 1024], BF16))) for i in range(2)]
            self.psi = 0
            self.pbi = 0
            gst = contextlib.ExitStack()
            k.pstack = gst
            self.ident = k.sb("ident", [128, 128], BF16)
            self.ones = k.sb("ones", [128, 128], BF16)
            k.dma("sp", self.ident.v(), self.dr["ident"].v())
            k.dma("sp", self.ones.v(), self.dr["ones"].v())
            k.pstack = None
            try:
                self.body()
            except StopIteration:
                pass
            k.barrier()
            gst.close()
        return nc

    def check(self, name):
        if self.stop_after == name:
            raise StopIteration

    def nps(self):
        p = self.ps[self.psi % 4]
        self.psi += 1
        return p

    def npb(self):
        p = self.pb[self.pbi % 2]
        self.pbi += 1
        return p

    def body(self):
        for l in range(DEPTH):
            self.emit_mod(l)
            self.check(f"mod{l}")
            for h in range(2):
                self.emit_norm(l, h, 1)
                self.emit_p1(l, h)
            self.check(f"p1_{l}")
            self.emit_kv(l)
            self.check(f"kv{l}")
            self.emit_mla(l)
            self.check(f"mla{l}")
            self.emit_na(l)
            self.check(f"na{l}")
            self.emit_fnet(l)
            self.check(f"fnet{l}")
            for h in range(2):
                self.emit_wout(l, h)
            self.check(f"wout{l}")
            for h in range(2):
                self.emit_norm(l, h, 2)
            self.check(f"norm2_{l}")
            for h in range(2):
                self.emit_ffn_up(l, h)
            self.check(f"up{l}")
            for h in range(2):
                self.emit_ffn_down(l, h)
            self.check(f"down{l}")
        self.emit_final()

    def tiles_of_half(self, l, h, with_ctx=True):
        ts = list(range(16 * h, 16 * h + 16))
        if with_ctx:
            ts.append(32 + h)
        return ts

    def xsrc(self, l):
        return self.dr["xin"] if l == 0 else self.dr["xs"]

    def bload(self, q, dst, row_ap):
        self.k.dma(q, dst, V(row_ap.partition_broadcast(128), []))

    def load_modB(self, l, seg, r, dst, base=None, plus1=False):
        k = self.k
        mv = self.dr["modv"]
        k.dma("sp", dst, V(mv.t[l, r, seg * D:(seg + 1) * D].partition_broadcast(128), mv.b(l)))
        if plus1:
            k.ts("pool", dst, dst, 1.0, None, ALU.add)
        if base is not None:
            k.tt("pool", dst, dst, base, ALU.mult)

    def rstd_of(self, ssq, rstd, width):
        k = self.k
        k.ts("dve", rstd, ssq, 1.0 / width, EPS, ALU.mult, ALU.add)
        k.act(rstd, rstd, AF.Sqrt)
        k.recip(rstd, rstd)

    def norm_tile(self, xt, ssq, rstd, junk, gB, shB, hb, width=D):
        k = self.k
        k.act(junk, xt, AF.Square, accum=ssq)
        self.rstd_of(ssq, rstd, width)
        if shB is None:
            k.stt(hb, xt, rstd, gB, ALU.mult, ALU.mult)
        else:
            k.stt(junk, xt, rstd, gB, ALU.mult, ALU.mult)
            hw = width // 2
            sl0, sl1 = np.s_[:, 0:hw], np.s_[:, hw:width]
            k.tt("pool", V(hb.ap[sl0], hb.bufs), V(junk.ap[sl0], junk.bufs), V(shB.ap[sl0], shB.bufs), ALU.add)
            k.tt("dve", V(hb.ap[sl1], hb.bufs), V(junk.ap[sl1], junk.bufs), V(shB.ap[sl1], shB.bufs), ALU.add)

    def transpose_to(self, src_tl, src_key, nchunk, dst_fn, rows=128):
        k = self.k
        j = 0
        while j < nchunk:
            n = min(4, nchunk - j)
            pb = self.npb()
            for i in range(n):
                k.tr(pb.v(np.s_[0:rows, i * 128:(i + 1) * 128]),
                     src_tl.v(np.s_[:, (j + i) * rows:(j + i + 1) * rows], keys=(src_key,)), self.ident.v())
            dst = dst_fn(j, n)
            k.copy("dve" if (nchunk == 16 and j == 12) else "act", dst, V(pb.t[0:rows, 0:n * 128].rearrange("p (n t) -> p n t", n=n), pb.b(0)))
            j += n

    def emit_mod(self, l):
        k = self.k
        dr = self.dr
        k.phase_begin()
        ccs = k.sb("ccs", [128, 16, 33], F32)
        ccb = k.sb("ccb", [128, 16, 33], BF16)
        brow = k.sb("brow", [33, 6 * D], F32)
        mrow = k.sb("mrow", [33, 6 * D], F32)
        wb = [k.sb(f"wmod{i}", [128, 16, 512], BF16) for i in range(2)]
        k.dma("sp", ccs.v(), dr["ccT"].v())
        k.act(ccb.v(), ccs.v(), AF.Silu)
        k.memset("dve", brow.v(), 0.0)
        k.dma("sp", brow.v(np.s_[0:1, :]), V(dr["b_mod"].t[l:l + 1, :], []))
        k.dma("sp", brow.v(np.s_[32:33, :]), V(dr["b_mod"].t[l:l + 1, :], []))
        for cb in range(24):
            w = wb[cb % 2]
            k.dma("pool", w.v(), V(dr["w_mod"].t[l, :, cb * 512:(cb + 1) * 512].rearrange("(j p) n -> p j n", p=128), []))
            ps = self.nps()
            for j in range(16):
                k.mm(ps.v(np.s_[0:33, :]), ccb.v(np.s_[:, j, :]), w.v(np.s_[:, j, :]), start=(j == 0), stop=(j == 15))
            k.tt("dve", mrow.v(np.s_[0:33, cb * 512:(cb + 1) * 512]), ps.v(np.s_[0:33, :]),
                 brow.v(np.s_[0:33, cb * 512:(cb + 1) * 512]), ALU.add)
        mv = dr["modv"]
        k.dma("sp", V(mv.t[l, 0:1, :], mv.b(l)), mrow.v(np.s_[0:1, :]))
        k.dma("sp", V(mv.t[l, 1:2, :], mv.b(l)), mrow.v(np.s_[32:33, :]))
        k.phase_end()

    def emit_p1(self, l, h):
        k = self.k
        dr = self.dr
        tiles = self.tiles_of_half(l, h)
        ntl = len(tiles)
        k.phase_begin()
        hT = k.sb("hT", [128, 16, ntl * 128], BF16)
        h2 = dr["h2T"]
        k.dma("sp", hT.v(np.s_[:, :, 0:2048], keys=tuple(range(16))), V(h2.t[:, h * 2048:(h + 1) * 2048].rearrange("(j p) t -> p j t", p=128), h2.b(h)))
        k.dma("sp", hT.v(np.s_[:, :, 2048:2176], keys=(16,)), V(h2.t[:, SEQ + h * 128:SEQ + (h + 1) * 128].rearrange("(j p) t -> p j t", p=128), h2.b(h)))
        junk = [k.sb(f"junk{i}", [128, 512], F32) for i in range(2)]
        st = [k.sb(f"st{i}", [128, 2], F32) for i in range(2)]
        gqB = k.sb("gqB", [128, 512], F32)
        gkvB = k.sb("gkvB", [128, 512], F32)
        self.bload("sp", gqB.v(), dr["g_q"].t[l, :])
        self.bload("sp", gkvB.v(), dr["g_kv"].t[l, :])
        cd = k.sb("cd", [128, 256], F32)
        wf = k.sb("wf", [128, 4, 128], F32)
        AB = k.sb("AB", [128, 4, 256], BF16)
        k.dma("sp", cd.v(), dr["cdft"].v())
        k.dma("sp", wf.v(), V(dr["w_fnet"].t[l].rearrange("g c d -> c g d"), []))
        for g in range(4):
            ps = self.nps()
            k.mm(ps.v(np.s_[:, 0:128]), cd.v(np.s_[:, 0:128]), wf.v(np.s_[:, g, :]))
            k.mm(ps.v(np.s_[:, 128:256]), cd.v(np.s_[:, 128:256]), wf.v(np.s_[:, g, :]))
            k.copy("dve", AB.v(np.s_[:, g, :]), ps.v(np.s_[:, 0:256]))
        wblk = [k.sb(f"wblk{i}", [128, 16, 512], BF16) for i in range(2)]
        stg = k.sb("stg", [128, 4, ntl * 128], BF16)
        ev = [k.sb(f"ev{i}", [128, 512], BF16) for i in range(2)]
        evx = [k.sb(f"evx{i}", [128, 1024], BF16) for i in range(2)]
        rt = [k.sb(f"rt{i}", [128, 128], F32) for i in range(2)]
        tmp = [k.sb(f"tmp{i}", [128, 192], F32) for i in range(2)]
        lat0 = h * 2048
        ctx0 = SEQ + h * 128
        for bi, (c0, cw) in enumerate(W_IN_BLOCKS):
            w = wblk[bi % 2]
            k.dma("pool", w.v(np.s_[:, :, 0:cw]), V(dr["w_in"].t[l, :, c0:c0 + cw].rearrange("(j p) n -> p j n", p=128), []))
            def stage_a(i, t, bi=bi, cw=cw, w=w):
                ps = self.nps()
                for j in range(16):
                    k.mm(ps.v(np.s_[:, 0:cw]), hT.v(np.s_[:, j, i * 128:(i + 1) * 128], keys=(i,)), w.v(np.s_[:, j, 0:cw]),
                         start=(j == 0), stop=(j == 15))
                e = ev[i % 2]
                s_ = st[i % 2]
                tr4 = lambda i=i, e=e: self.transpose_to(e, 0, 4, lambda j0, n: stg.v(np.s_[:, j0:j0 + n, i * 128:(i + 1) * 128], keys=(i,)))
                if bi in (0, 1):
                    gg = gqB if bi == 0 else gkvB
                    k.act(junk[i % 2].v(np.s_[:, 0:512]), ps.v(), AF.Square, accum=s_.v(np.s_[:, 0:1]))
                    self.rstd_of(s_.v(np.s_[:, 0:1]), s_.v(np.s_[:, 1:2]), 512)
                    k.stt(e.v(), ps.v(), s_.v(np.s_[:, 1:2]), gg.v(), ALU.mult, ALU.mult)
                    return tr4
                elif bi == 2:
                    r_ = rt[i % 2]
                    k.dma("sp", r_.v(), V(dr["ropeTM"].t[t * 128:(t + 1) * 128, :], []))
                    tm = tmp[i % 2]
                    k.tt("dve", tm.v(np.s_[:, 0:64]), ps.v(np.s_[:, 0:64]), r_.v(np.s_[:, 0:64]), ALU.mult)
                    pv = ps.t[:, 0:64].rearrange("p (a s n) -> p a s n", a=2, s=2)
                    cv = r_.t[:, 64:128].rearrange("p (a s n) -> p a s n", a=2, s=2)
                    ov = tm.t[:, 64:128].rearrange("p (a s n) -> p a s n", a=2, s=2)
                    for s2 in range(2):
                        k.tt("dve", V(ov[:, :, s2, :], tm.b(0)), V(pv[:, :, 1 - s2, :], ps.b(0)), V(cv[:, :, s2, :], r_.b(0)), ALU.mult)
                    k.tt("dve", e.v(np.s_[:, 0:64]), tm.v(np.s_[:, 0:64]), tm.v(np.s_[:, 64:128]), ALU.add)
                    return lambda i=i, e=e: self.transpose_to(e, 0, 1, lambda j0, n: stg.v(np.s_[0:64, 0:1, i * 128:(i + 1) * 128], keys=(i,)), rows=64)
                elif bi in (3, 4):
                    k.copy("act", e.v(), ps.v())
                    return tr4
                elif bi == 5:
                    k.copy("act", e.v(), ps.v())
                    return lambda t=t, e=e: k.dma("sp", V(dr["vn"].t[t * 128:(t + 1) * 128, :], dr["vn"].b(t)), e.v())
                else:
                    k.copy("act", e.v(), ps.v())

                    def fb(i=i, t=t, e=e):
                        tr4()
                        ex = evx[i % 2]
                        for g2 in range(2):
                            p2 = self.nps()
                            for gg_ in range(2):
                                g = g2 * 2 + gg_
                                k.mm(p2.v(np.s_[:, gg_ * 256:(gg_ + 1) * 256]), stg.v(np.s_[:, g, i * 128:(i + 1) * 128], keys=(i,)),
                                     AB.v(np.s_[:, g, :]))
                            k.copy("dve", ex.v(np.s_[:, g2 * 512:(g2 + 1) * 512]), p2.v())
                        k.dma("sp", V(dr["xab"].t[t * 128:(t + 1) * 128, :], dr["xab"].b(t)), ex.v())
                    return fb

            pend = None
            for i, t in enumerate(tiles):
                nb = stage_a(i, t)
                if pend is not None:
                    pend()
                pend = nb
            pend()
            name = {0: "cqnT", 1: "ckvnT", 2: "krT", 3: "qnT", 4: "knT"}.get(bi)
            if name is not None:
                dt_ = dr[name]
                allk = tuple(range(ntl))
                if bi == 2:
                    k.dma("sp", V(dt_.t[:, lat0:lat0 + 2048], dt_.b(h)), stg.v(np.s_[0:64, 0, 0:2048], keys=allk))
                    k.dma("sp", V(dt_.t[:, ctx0:ctx0 + 128], dt_.b(h)), stg.v(np.s_[0:64, 0, 2048:2176], keys=allk))
                else:
                    k.dma("sp", V(dt_.t[:, lat0:lat0 + 2048].rearrange("(j p) t -> p j t", p=128), dt_.b(h)),
                          stg.v(np.s_[:, :, 0:2048], keys=allk))
                    k.dma("sp", V(dt_.t[:, ctx0:ctx0 + 128].rearrange("(j p) t -> p j t", p=128), dt_.b(h)),
                          stg.v(np.s_[:, :, 2048:2176], keys=allk))
        k.phase_end()

    def emit_kv(self, l):
        k = self.k
        dr = self.dr
        k.phase_begin()
        ck = k.sb("ck", [128, 4, NTOK], BF16)
        wk = k.sb("wk", [128, 4, 2048], BF16)
        k.dma("sp", ck.v(), V(dr["ckvnT"].t.rearrange("(j p) t -> p j t", p=128), dr["ckvnT"].b(0, 1)))
        k.dma("pool", wk.v(), V(dr["w_ukv"].t[l].rearrange("(j p) n -> p j n", p=128), []))
        kst = [k.sb(f"kst{i}", [128, NTOK], BF16) for i in range(2)]
        groups = [(g * 512, 512) for g in range(8)] + [(4096, 256)]
        for hd in range(8):
            ks_ = kst[hd % 2]
            for gi, (g0, gn) in enumerate(groups):
                ps = self.nps()
                for j in range(4):
                    k.mm(ps.v(np.s_[:, 0:gn]), wk.v(np.s_[:, j, hd * 256:hd * 256 + 128]), ck.v(np.s_[:, j, g0:g0 + gn]),
                         start=(j == 0), stop=(j == 3))
                k.copy("act" if gi % 2 == 0 else "dve", ks_.v(np.s_[:, g0:g0 + gn]), ps.v(np.s_[:, 0:gn]))
            k.dma("sp", V(dr["kTn"].t[hd], dr["kTn"].b(hd)), ks_.v())
        vst = [k.sb(f"vst{i}", [128, 1024], BF16) for i in range(2)]
        wkv = wk.t[:].rearrange("p j (h c) -> p j h c", h=8)
        for t in range(NT):
            vs_ = vst[t % 2]
            for half in range(2):
                ps = self.nps()
                for j in range(4):
                    k.mm(V(ps.t[:].rearrange("p (h c) -> p h c", h=4), ps.b(0)), ck.v(np.s_[:, j, t * 128:(t + 1) * 128]),
                         V(wkv[:, j, half * 4:(half + 1) * 4, 128:256], wk.b(0)), start=(j == 0), stop=(j == 3))
                k.copy("act" if half == 0 else "dve", vs_.v(np.s_[:, half * 512:(half + 1) * 512]), ps.v())
            k.dma("sp", V(dr["vall"].t[t * 128:(t + 1) * 128, :], dr["vall"].b(0)), vs_.v())
        k.phase_end()

    def attn_block(self, qparts, kparts, vfn, ktiles, nq, out_v, bias_fn=None, scale=1.0, den_dve=False):
        k = self.k
        ot = self.ps[4]
        den = self.ps[5]
        nk = len(ktiles)
        pts = [None] * nk
        SK = 2

        def stage1(ki):
            kt = ktiles[ki]
            st = self.nps()
            kp = kparts(kt)
            for pi, (qa, ka) in enumerate(zip(qparts, kp)):
                k.mm(st.v(np.s_[:, 0:nq]), ka, qa, start=(pi == 0), stop=(pi == len(kp) - 1), sig=(pi == len(kp) - 1))
            pt = self.ptb[self.pti % 4]
            self.pti += 1
            bb = bias_fn(kt) if bias_fn is not None else None
            if bb is not None:
                sb_ = self.sbb[self.pti % 2]
                k.stt(sb_.v(np.s_[:, 0:nq]), st.v(np.s_[:, 0:nq]), scale, bb, ALU.mult, ALU.add)
                k.act(pt.v(np.s_[:, 0:nq]), sb_.v(np.s_[:, 0:nq]), AF.Exp)
            else:
                k.act(pt.v(np.s_[:, 0:nq]), st.v(np.s_[:, 0:nq]), AF.Exp, scale=scale)
            pts[ki] = pt

        def stage2(ki):
            kt = ktiles[ki]
            pt = pts[ki]
            if den_dve:
                k.mm(ot.v(np.s_[:, 0:nq]), vfn(kt), pt.v(np.s_[:, 0:nq]), start=(ki == 0), stop=(ki == nk - 1), sig=True)
                if ki == 0:
                    k.copy("dve", self.accb.v(np.s_[:, 0:nq]), pt.v(np.s_[:, 0:nq]))
                else:
                    k.tt("dve", self.accb.v(np.s_[:, 0:nq]), self.accb.v(np.s_[:, 0:nq]), pt.v(np.s_[:, 0:nq]), ALU.add)
                if ki == nk - 1:
                    k.mm(den.v(np.s_[:, 0:nq]), self.onesf.v(), self.accb.v(np.s_[:, 0:nq]))
            else:
                k.mm(ot.v(np.s_[:, 0:nq]), vfn(kt), pt.v(np.s_[:, 0:nq]), start=(ki == 0), stop=(ki == nk - 1), sig=False)
                k.mm(den.v(np.s_[:, 0:nq]), self.ones.v(), pt.v(np.s_[:, 0:nq]), start=(ki == 0), stop=(ki == nk - 1), sig=True)

        for i in range(nk + SK):
            if i < nk:
                stage1(i)
            if i - SK >= 0:
                stage2(i - SK)
        rd = self.rdb[self.pti % 2]
        k.recip(rd.v(np.s_[:, 0:nq]), den.v(np.s_[:, 0:nq]))
        k.tt("dve", out_v, ot.v(np.s_[:, 0:nq]), rd.v(np.s_[:, 0:nq]), ALU.mult)

    def attn_bufs(self):
        k = self.k
        self.ptb = [k.sb(f"pt{i}", [128, 512], BF16) for i in range(4)]
        self.sbb = [k.sb(f"sbb{i}", [128, 512], F32) for i in range(2)]
        self.rdb = [k.sb(f"rd{i}", [128, 512], F32) for i in range(2)]
        self.accb = k.sb("accb", [128, 512], F32)
        self.onesf = k.sb("onesf", [128, 128], F32)
        k.memset("pool", self.onesf.v(), 1.0)
        self.pti = 0

    def emit_mla(self, l):
        k = self.k
        dr = self.dr
        nqtok = NTOK if l == 0 else SEQ
        k.phase_begin()
        self.attn_bufs()
        va = k.sb("va", [128, NT, 1024], BF16)
        k.dma("sp", va.v(), V(dr["vall"].t.rearrange("(t p) c -> p t c", p=128), dr["vall"].b(0)))
        kr = k.sb("kr", [64, NTOK], BF16)
        k.dma("sp", kr.v(), V(dr["krT"].t[:, :], dr["krT"].b(0, 1)))
        cq = [k.sb(f"cq{i}", [128, 4, 512], BF16) for i in range(2)]
        c1 = [k.sb(f"c1{i}", [64, 512], F32) for i in range(2)]
        c2 = [k.sb(f"c2{i}", [64, 512], F32) for i in range(2)]
        kh = [k.sb(f"kh{i}", [128, NTOK], BF16) for i in range(2)]
        wq = [k.sb(f"wq{i}", [128, 4, 256], BF16) for i in range(2)]
        qn_ = k.sb("qn", [128, NTOK], BF16)
        qr_ = k.sb("qr", [64, NTOK], BF16)
        t1 = [k.sb(f"t1{i}", [64, 512], F32) for i in range(2)]
        t2 = [k.sb(f"t2{i}", [64, 512], F32) for i in range(2)]
        os_ = k.sb("ost", [128, NTOK], BF16)
        qgroups = [(g * 512, 512) for g in range(8)] + ([(4096, 256)] if l == 0 else [])
        cqv = dr["cqnT"].t.rearrange("(j p) t -> p j t", p=128)
        it = 0
        def load_head(hd):
            kh_, wq_ = kh[hd % 2], wq[hd % 2]
            k.dma("sp", kh_.v(), V(dr["kTn"].t[hd], dr["kTn"].b(hd)))
            k.dma("pool", wq_.v(np.s_[:, :, 0:192]), V(dr["w_uq"].t[l, :, hd * 192:(hd + 1) * 192].rearrange("(j p) n -> p j n", p=128), []))
            k.dma("pool", wq_.v(np.s_[:, :, 192:256]), V(dr["w_uqs"].t[l, :, hd * 64:(hd + 1) * 64].rearrange("(j p) n -> p j n", p=128), []))

        load_head(0)
        for hd in range(8):
            kh_, wq_ = kh[hd % 2], wq[hd % 2]
            if hd + 1 < 8:
                load_head(hd + 1)
            for gi, (g0, gn) in enumerate(qgroups):
                cq_, c1_, c2_ = cq[it % 2], c1[it % 2], c2[it % 2]
                a_, b_ = t1[it % 2], t2[it % 2]
                it += 1
                k.dma("sp", cq_.v(np.s_[:, :, 0:gn]), V(cqv[:, :, g0:g0 + gn], dr["cqnT"].b(0, 1)))
                k.dma("sp", c1_.v(np.s_[:, 0:gn]), V(dr["ropeC1"].t[:, g0:g0 + gn], []))
                k.dma("sp", c2_.v(np.s_[:, 0:gn]), V(dr["ropeC2"].t[:, g0:g0 + gn], []))
                ps = self.nps()
                for j in range(4):
                    k.mm(ps.v(np.s_[:, 0:gn]), wq_.v(np.s_[:, j, 0:128]), cq_.v(np.s_[:, j, 0:gn]), start=(j == 0), stop=(j == 3))
                k.act(qn_.v(np.s_[:, g0:g0 + gn]), ps.v(np.s_[:, 0:gn]), AF.Copy, scale=MLA_SCALE)
                pa = self.nps()
                for j in range(4):
                    k.mm(pa.v(np.s_[0:64, 0:gn]), wq_.v(np.s_[:, j, 128:192]), cq_.v(np.s_[:, j, 0:gn]), start=(j == 0), stop=(j == 3))
                pb_ = self.nps()
                for j in range(4):
                    k.mm(pb_.v(np.s_[0:64, 0:gn]), wq_.v(np.s_[:, j, 192:256]), cq_.v(np.s_[:, j, 0:gn]), start=(j == 0), stop=(j == 3))
                k.tt("dve", a_.v(np.s_[:, 0:gn]), pa.v(np.s_[0:64, 0:gn]), c1_.v(np.s_[:, 0:gn]), ALU.mult)
                k.stt(b_.v(np.s_[:, 0:gn]), pb_.v(np.s_[0:64, 0:gn]), MLA_SCALE, c2_.v(np.s_[:, 0:gn]), ALU.mult, ALU.mult)
                k.stt(qr_.v(np.s_[:, g0:g0 + gn]), a_.v(np.s_[:, 0:gn]), MLA_SCALE, b_.v(np.s_[:, 0:gn]), ALU.mult, ALU.add)
            for gi, (g0, gn) in enumerate(qgroups):
                ktl = list(range(NT)) if g0 < SEQ else [32, 33]
                self.attn_block(
                    [qn_.v(np.s_[:, g0:g0 + gn]), qr_.v(np.s_[:, g0:g0 + gn])],
                    lambda kt, kh_=kh_: [kh_.v(np.s_[:, kt * 128:(kt + 1) * 128]), kr.v(np.s_[:, kt * 128:(kt + 1) * 128])],
                    lambda kt, hd=hd: va.v(np.s_[:, kt, hd * 128:(hd + 1) * 128]),
                    ktl, gn, os_.v(np.s_[:, g0:g0 + gn]), den_dve=False)
            mx = dr["mixT"]
            k.dma("pool", V(mx.t[hd * 128:(hd + 1) * 128, 0:nqtok], mx.b(hd)), os_.v(np.s_[:, 0:nqtok]))
        k.phase_end()

    def emit_na(self, l):
        k = self.k
        dr = self.dr
        nqtok = NTOK if l == 0 else SEQ
        k.phase_begin()
        self.attn_bufs()
        vn = k.sb("vn", [128, NT, 512], BF16)
        k.dma("sp", vn.v(), V(dr["vn"].t.rearrange("(t p) c -> p t c", p=128), dr["vn"].b(*range(NT))))
        qh = [k.sb(f"qh{i}", [128, NTOK], BF16) for i in range(2)]
        kh = [k.sb(f"kh{i}", [128, NTOK], BF16) for i in range(2)]
        ost = [k.sb(f"ost{i}", [128, NTOK], BF16) for i in range(2)]
        bt = [k.sb(f"bt{i}", [128, 512], F32) for i in range(16)]
        bti = 0
        for hd in range(4):
            qh_, kh_, os_ = qh[hd % 2], kh[hd % 2], ost[hd % 2]
            k.dma("sp", qh_.v(), V(dr["qnT"].t[hd * 128:(hd + 1) * 128, :], dr["qnT"].b(0, 1)))
            k.dma("sp", kh_.v(), V(dr["knT"].t[hd * 128:(hd + 1) * 128, :], dr["knT"].b(0, 1)))
            for s in range(8):
                var = 0 if s == 0 else (2 if s == 7 else 1)
                kts = []
                btl = {}
                for wt in range(8):
                    kt = 4 * s - 2 + wt
                    if 0 <= kt < 32:
                        b_ = bt[bti % 16]
                        bti += 1
                        k.dma("sp", b_.v(), V(dr["nab"].t[l, hd, var, wt], []))
                        btl[kt] = b_.v()
                        kts.append(kt)
                kts += [32, 33]
                self.attn_block(
                    [qh_.v(np.s_[:, s * 512:(s + 1) * 512])],
                    lambda kt, kh_=kh_: [kh_.v(np.s_[:, kt * 128:(kt + 1) * 128])],
                    lambda kt, hd=hd: vn.v(np.s_[:, kt, hd * 128:(hd + 1) * 128]),
                    kts, 512, os_.v(np.s_[:, s * 512:(s + 1) * 512]),
                    bias_fn=lambda kt, btl=btl: btl.get(kt), scale=NA_SCALE)
            if l == 0:
                self.attn_block(
                    [qh_.v(np.s_[:, SEQ:NTOK])],
                    lambda kt, kh_=kh_: [kh_.v(np.s_[:, kt * 128:(kt + 1) * 128])],
                    lambda kt, hd=hd: vn.v(np.s_[:, kt, hd * 128:(hd + 1) * 128]),
                    [32, 33], 256, os_.v(np.s_[:, SEQ:NTOK]), scale=NA_SCALE)
            mx = dr["mixT"]
            k.dma("pool", V(mx.t[1024 + hd * 128:1024 + (hd + 1) * 128, 0:nqtok], mx.b(8 + hd)), os_.v(np.s_[:, 0:nqtok]))
        k.phase_end()

    def emit_fnet(self, l):
        k = self.k
        dr = self.dr
        nqtok = NTOK if l == 0 else SEQ
        k.phase_begin()
        xa = k.sb("xa", [128, NT, 1024], BF16)
        k.dma("sp", xa.v(), V(dr["xab"].t.rearrange("(t p) c -> p t c", p=128), dr["xab"].b(*range(NT))))
        tb = [k.sb(f"tb{i}", [128, 8, 512], BF16) for i in range(4)]
        ost = [k.sb(f"ost{g}", [128, NTOK], BF16) for g in range(4)]
        tbi = 0
        for s in range(8):
            pss = [self.nps() for g in range(4)]
            step = 0
            for tab in ("dftC", "dftS"):
                off = 0 if tab == "dftC" else 128
                for pc in range(4):
                    t_ = tb[tbi % 4]
                    tbi += 1
                    k.dma("sp", t_.v(), V(dr[tab].t[s, pc], []))
                    for g in range(4):
                        for jj in range(8):
                            jc = pc * 8 + jj
                            k.mm(pss[g].v(), xa.v(np.s_[:, jc, g * 256 + off:g * 256 + off + 128]), t_.v(np.s_[:, jj, :]),
                                 start=(step == 0 and jj == 0), stop=(step == 7 and jj == 7), sig=(jj == 7))
                    step += 1
            for g in range(4):
                k.copy("act" if g % 2 == 0 else "dve", ost[g].v(np.s_[:, s * 512:(s + 1) * 512]), pss[g].v())
        if l == 0:
            tc_ = k.sb("tcc", [128, 2, 2, 256], BF16)
            k.dma("sp", tc_.v(np.s_[:, 0]), V(dr["dftCc"].t.rearrange("(j p) n -> p j n", p=128), []))
            k.dma("sp", tc_.v(np.s_[:, 1]), V(dr["dftSc"].t.rearrange("(j p) n -> p j n", p=128), []))
            for g in range(4):
                ps = self.nps()
                step = 0
                for ti in range(2):
                    for jj in range(2):
                        k.mm(ps.v(np.s_[:, 0:256]), xa.v(np.s_[:, 32 + jj, g * 256 + ti * 128:g * 256 + ti * 128 + 128]),
                             tc_.v(np.s_[:, ti, jj, :]), start=(step == 0), stop=(step == 3))
                        step += 1
                k.copy("act", ost[g].v(np.s_[:, SEQ:NTOK]), ps.v(np.s_[:, 0:256]))
        mx = dr["mixT"]
        for g in range(4):
            k.dma("sp", V(mx.t[1536 + g * 128:1536 + (g + 1) * 128, 0:nqtok], mx.b(12 + g)), ost[g].v(np.s_[:, 0:nqtok]))
        k.phase_end()

    def emit_wout(self, l, h):
        k = self.k
        dr = self.dr
        tiles = self.tiles_of_half(l, h, with_ctx=(l == 0))
        k.phase_begin()
        mT = k.sb("mT", [128, 16, 17 * 128], BF16)
        mx = dr["mixT"]
        allb = mx.b(*range(16))
        k.dma("sp", mT.v(np.s_[:, :, 0:2048]), V(mx.t[:, h * 2048:(h + 1) * 2048].rearrange("(j p) t -> p j t", p=128), allb))
        if l == 0:
            k.dma("sp", mT.v(np.s_[:, :, 2048:2176]), V(mx.t[:, SEQ + h * 128:SEQ + (h + 1) * 128].rearrange("(j p) t -> p j t", p=128), allb))
        gt = [k.sb(f"gt{r}", [128, D], F32) for r in range(2)]
        for r in range(2):
            self.load_modB(l, 2, r, gt[r].v())
        self.proj_residual(l, tiles, mT, 16, dr["w_out"].t[l], gt, self.xsrc(l))
        k.phase_end()

    def proj_residual(self, l, tiles, aT, nch, w_ap, gt, xsrc, a_loader=None):
        k = self.k
        dr = self.dr
        wb = [k.sb(f"wpr{i}", [128, nch, 512], BF16) for i in range(2)]
        xt = [k.sb(f"xpr{i}", [128, 512], F32) for i in range(3)]
        tp = [k.sb(f"tpr{i}", [128, 512], F32) for i in range(2)]
        xs = dr["xs"]
        half_n = nch // 2
        for blk in range(4):
            w = wb[blk % 2]
            c0 = blk * 512
            k.dma("pool", w.v(np.s_[:, 0:half_n, :], keys=(0,)), V(w_ap[0:half_n * 128, c0:c0 + 512].rearrange("(j p) n -> p j n", p=128), []))
            k.dma("pool", w.v(np.s_[:, half_n:nch, :], keys=(1,)), V(w_ap[half_n * 128:nch * 128, c0:c0 + 512].rearrange("(j p) n -> p j n", p=128), []))
            for i, t in enumerate(tiles):
                r = 1 if t >= 32 else 0
                if a_loader is not None:
                    a_v = a_loader(blk, i, t)
                else:
                    a_v = lambda j, i=i: aT.v(np.s_[:, j, i * 128:(i + 1) * 128])
                ps = self.nps()
                for j in range(nch):
                    k.mm(ps.v(), a_v(j), w.v(np.s_[:, j, :], keys=(0 if j < half_n else 1,)), start=(j == 0), stop=(j == nch - 1))
                x_ = xt[i % 3]
                k.dma("sp", x_.v(), V(xsrc.t[t * 128:(t + 1) * 128, c0:c0 + 512], xsrc.b(t)))
                t_ = tp[i % 2]
                k.tt("dve", t_.v(), ps.v(), gt[r].v(np.s_[:, c0:c0 + 512]), ALU.mult)
                k.tt("pool", x_.v(), t_.v(), x_.v(), ALU.add)
                k.dma("act", V(xs.t[t * 128:(t + 1) * 128, c0:c0 + 512], xs.b(t)), x_.v())

    def emit_norm(self, l, h, which):
        k = self.k
        dr = self.dr
        if which == 1:
            tiles = self.tiles_of_half(l, h, True)
            gname, sg_, ss_, xsrc = "g_attn", 1, 0, self.xsrc(l)
        else:
            tiles = self.tiles_of_half(l, h, with_ctx=(l == 0))
            gname, sg_, ss_, xsrc = "g_ffn", 4, 3, dr["xs"]
        ntl = len(tiles)
        k.phase_begin()
        gA = k.sb("gA", [128, D], F32)
        gB = [k.sb(f"gB{r}", [128, D], F32) for r in range(2)]
        shB = [k.sb(f"shB{r}", [128, D], F32) for r in range(2)]
        self.bload("sp", gA.v(), dr[gname].t[l, :])
        for r in range(2):
            self.load_modB(l, sg_, r, gB[r].v(), base=gA.v(), plus1=True)
            self.load_modB(l, ss_, r, shB[r].v())
        xt = [k.sb(f"xt{i}", [128, D], F32) for i in range(2)]
        junk = [k.sb(f"junk{i}", [128, D], F32) for i in range(2)]
        hb = [k.sb(f"hb{i}", [128, D], BF16) for i in range(2)]
        st = [k.sb(f"st{i}", [128, 2], F32) for i in range(2)]
        hT = k.sb("hT", [128, 16, 17 * 128], BF16)
        pend = None
        for i, t in enumerate(tiles):
            r = 1 if t >= 32 else 0
            x_ = xt[i % 2]
            k.dma("sp", x_.v(), V(xsrc.t[t * 128:(t + 1) * 128, :], xsrc.b(t)))
            s_ = st[i % 2]
            self.norm_tile(x_.v(), s_.v(np.s_[:, 0:1]), s_.v(np.s_[:, 1:2]), junk[i % 2].v(), gB[r].v(), shB[r].v(), hb[i % 2].v())
            if pend is not None:
                pend()
            pend = lambda i=i: self.transpose_to(hb[i % 2], 0, 16, lambda j0, n: hT.v(np.s_[:, j0:j0 + n, i * 128:(i + 1) * 128], keys=(i,)))
        pend()
        h2 = dr["h2T"]
        allk = tuple(range(ntl))
        k.dma("sp", V(h2.t[:, h * 2048:(h + 1) * 2048].rearrange("(j p) t -> p j t", p=128), h2.b(h)), hT.v(np.s_[:, :, 0:2048], keys=allk))
        if ntl == 17:
            k.dma("sp", V(h2.t[:, SEQ + h * 128:SEQ + (h + 1) * 128].rearrange("(j p) t -> p j t", p=128), h2.b(h)),
                  hT.v(np.s_[:, :, 2048:2176], keys=allk))
        k.phase_end()

    def emit_ffn_up(self, l, h):
        k = self.k
        dr = self.dr
        k.phase_begin()
        W = 2048 + 2 + 128 + 2
        hx = k.sb("hx", [128, 16, W], BF16)
        h2 = dr["h2T"]
        h2v = h2.t.rearrange("(j p) t -> p j t", p=128)
        bb = h2.b(0, 1)
        k.memset("pool", hx.v(), 0.0)
        lo = h * 2048 - 1
        hi = h * 2048 + 2049
        slo, shi = max(lo, 0), min(hi, SEQ)
        k.dma("sp", hx.v(np.s_[:, :, slo - lo:slo - lo + (shi - slo)]), V(h2v[:, :, slo:shi], bb))
        if l == 0:
            lo = SEQ + h * 128 - 1
            hi = SEQ + h * 128 + 129
            slo, shi = max(lo, SEQ), min(hi, NTOK)
            k.dma("sp", hx.v(np.s_[:, :, 2050 + slo - lo:2050 + slo - lo + (shi - slo)]), V(h2v[:, :, slo:shi], bb))
        groups = [(1 + 410 * i, 410) for i in range(4)] + [(1641, 408)]
        if l == 0:
            groups.append((2051, 128))
        cp = k.sb("cp", [128, 88, 4], F32)
        k.dma("sp", cp.v(), V(dr["convp"].t[l], []))
        wg = [k.sb(f"wg{i}", [128, 16, 512], BF16) for i in range(2)]
        wv = [k.sb(f"wv{i}", [128, 16, 512], BF16) for i in range(2)]
        ast = [k.sb(f"ast{i}", [128, 2176], BF16) for i in range(2)]
        tg = [k.sb(f"tg{i}", [128, 410], F32) for i in range(2)]
        tv = [k.sb(f"tv{i}", [128, 410], F32) for i in range(2)]
        sg = [k.sb(f"sg{i}", [128, 410], F32) for i in range(2)]
        aT = dr["aT"]
        it = 0
        for c4 in range(11):
            g_, v_ = wg[c4 % 2], wv[c4 % 2]
            k.dma("pool", g_.v(), V(dr["w_up"].t[l, :, c4 * 512:(c4 + 1) * 512].rearrange("(j p) n -> p j n", p=128), []))
            k.dma("pool", v_.v(), V(dr["w_up"].t[l, :, DFF + c4 * 512:DFF + (c4 + 1) * 512].rearrange("(j p) n -> p j n", p=128), []))
            for cc in range(4):
                c = c4 * 4 + cc
                a_ = ast[c % 2]
                for (g0, gn) in groups:
                    pg = self.nps()
                    pv = self.nps()
                    for j in range(16):
                        k.mm(pg.v(np.s_[:, 0:gn + 2]), g_.v(np.s_[:, j, cc * 128:(cc + 1) * 128]), hx.v(np.s_[:, j, g0 - 1:g0 + gn + 1]),
                             start=(j == 0), stop=(j == 15))
                    for j in range(16):
                        k.mm(pv.v(np.s_[:, 0:gn + 2]), v_.v(np.s_[:, j, cc * 128:(cc + 1) * 128]), hx.v(np.s_[:, j, g0 - 1:g0 + gn + 1]),
                             start=(j == 0), stop=(j == 15))
                    tg_, tv_, sg_ = tg[it % 2], tv[it % 2], sg[it % 2]
                    it += 1
                    for (pp, tt_, ci) in ((pg, tg_, c), (pv, tv_, 44 + c)):
                        k.act(tt_.v(np.s_[:, 0:gn]), pp.v(np.s_[:, 0:gn]), AF.Identity, bias=cp.v(np.s_[:, ci, 3:4]), scale=cp.v(np.s_[:, ci, 0:1]))
                        k.stt(tt_.v(np.s_[:, 0:gn]), pp.v(np.s_[:, 1:gn + 1]), cp.v(np.s_[:, ci, 1:2]), tt_.v(np.s_[:, 0:gn]), ALU.mult, ALU.add)
                        k.stt(tt_.v(np.s_[:, 0:gn]), pp.v(np.s_[:, 2:gn + 2]), cp.v(np.s_[:, ci, 2:3]), tt_.v(np.s_[:, 0:gn]), ALU.mult, ALU.add)
                    k.act(sg_.v(np.s_[:, 0:gn]), tg_.v(np.s_[:, 0:gn]), AF.Silu)
                    o0 = g0 - 1 if g0 < 2050 else 2048 + (g0 - 2051)
                    k.tt("pool", a_.v(np.s_[:, o0:o0 + gn]), sg_.v(np.s_[:, 0:gn]), tv_.v(np.s_[:, 0:gn]), ALU.mult)
                k.dma("sp", V(aT.t[c * 128:(c + 1) * 128, h * 2048:(h + 1) * 2048], aT.b(c)), a_.v(np.s_[:, 0:2048]))
                if l == 0:
                    k.dma("sp", V(aT.t[c * 128:(c + 1) * 128, SEQ + h * 128:SEQ + (h + 1) * 128], aT.b(c)), a_.v(np.s_[:, 2048:2176]))
        k.phase_end()

    def emit_ffn_down(self, l, h):
        k = self.k
        dr = self.dr
        tiles = self.tiles_of_half(l, h, with_ctx=(l == 0))
        k.phase_begin()
        gt = [k.sb(f"gt{r}", [128, D], F32) for r in range(2)]
        for r in range(2):
            self.load_modB(l, 5, r, gt[r].v())
        ab = [k.sb(f"ab{i}", [128, 44, 128], BF16) for i in range(3)]
        aT = dr["aT"]
        aTv = aT.t.rearrange("(j p) t -> p j t", p=128)
        allb = aT.b(*range(44))
        cnt = [0]

        def loader(blk, i, t):
            a_ = ab[cnt[0] % 3]
            cnt[0] += 1
            k.dma("sp", a_.v(), V(aTv[:, :, t * 128:(t + 1) * 128], allb))
            return lambda j, a_=a_: a_.v(np.s_[:, j, :])

        self.proj_residual(l, tiles, None, 44, dr["w_down"].t[l], gt, dr["xs"], a_loader=loader)
        k.phase_end()

    def emit_final(self):
        k = self.k
        dr = self.dr
        k.phase_begin()
        gF = k.sb("gF", [128, D], F32)
        self.bload("sp", gF.v(), dr["g_final"].t[:])
        xt = [k.sb(f"xt{i}", [128, D], F32) for i in range(2)]
        junk = [k.sb(f"junk{i}", [128, D], F32) for i in range(2)]
        ot = [k.sb(f"ot{i}", [128, D], F32) for i in range(2)]
        st = [k.sb(f"st{i}", [128, 2], F32) for i in range(2)]
        xs = dr["xs"]
        for t in range(32):
            x_, s_ = xt[t % 2], st[t % 2]
            k.dma("sp", x_.v(), V(xs.t[t * 128:(t + 1) * 128, :], xs.b(t)))
            self.norm_tile(x_.v(), s_.v(np.s_[:, 0:1]), s_.v(np.s_[:, 1:2]), junk[t % 2].v(), gF.v(), None, ot[t % 2].v())
            k.dma("pool", V(dr["out"].t[t * 128:(t + 1) * 128, :], dr["out"].b(t)), ot[t % 2].v())
        k.phase_end()


def _tables():
    bf = ml_dtypes.bfloat16
    tb = {}
    tb["ident"] = np.eye(128, dtype=np.float32).astype(bf)
    tb["ones"] = np.ones((128, 128), np.float32).astype(bf)
    pos = np.arange(SEQ)
    n = 16
    freqs = (10000.0 ** (-np.arange(n, dtype=np.float32) / n)).astype(np.float32)
    ang_r = (pos // 64).astype(np.float32)[:, None] * freqs
    ang_c = (pos % 64).astype(np.float32)[:, None] * freqs
    cr, sr, cc, sc = np.cos(ang_r), np.sin(ang_r), np.cos(ang_c), np.sin(ang_c)
    c1 = np.concatenate([cr, cr, cc, cc], axis=1).astype(np.float32)
    c2 = np.concatenate([-sr, sr, -sc, sc], axis=1).astype(np.float32)
    c1 = np.concatenate([c1, np.ones((CTX, 64), np.float32)], axis=0)
    c2 = np.concatenate([c2, np.zeros((CTX, 64), np.float32)], axis=0)
    tb["ropeTM"] = np.ascontiguousarray(np.concatenate([c1, c2], axis=1))
    tb["ropeC1"] = np.ascontiguousarray(c1.T)
    tb["ropeC2"] = np.ascontiguousarray(c2.T)
    j = np.arange(SEQ, dtype=np.int64)
    m = (j[:, None] * j[None, :]) % SEQ
    ang = (2.0 * np.pi / SEQ) * m
    def tl(a):
        return np.ascontiguousarray(a.reshape(4, 8, 128, 8, 512).transpose(3, 0, 2, 1, 4))
    tb["dftC"] = tl((np.cos(ang) / 64.0).astype(np.float32).astype(bf))
    tb["dftS"] = tl((-np.sin(ang) / 64.0).astype(np.float32).astype(bf))
    j = np.arange(CTX, dtype=np.int64)
    ang = (2.0 * np.pi / CTX) * ((j[:, None] * j[None, :]) % CTX)
    tb["dftCc"] = (np.cos(ang) / 16.0).astype(np.float32).astype(bf)
    tb["dftSc"] = (-np.sin(ang) / 16.0).astype(np.float32).astype(bf)
    j = np.arange(128, dtype=np.int64)
    ang = (2.0 * np.pi / 128) * ((j[:, None] * j[None, :]) % 128)
    s128 = np.sqrt(128.0)
    tb["cdft"] = np.ascontiguousarray(np.concatenate([np.cos(ang) / s128, np.sin(ang) / s128], axis=1).astype(np.float32))
    return tb


def _na_bias(na_rpb):
    L = na_rpb.shape[0]
    out = np.full((L, 4, 3, 8, 2, 64, 8, 64), NEG, np.float32)
    w = np.arange(64)
    col_start = np.clip(w - 8, 0, 48)
    jj = np.arange(64)
    inwin = (jj[None, :] >= col_start[:, None]) & (jj[None, :] < col_start[:, None] + 16)
    relc = jj[None, :] - w[:, None] + 15
    relc_c = np.clip(relc, 0, 30)
    for var, s in ((0, 0), (1, 3), (2, 7)):
        for wt in range(8):
            kt = 4 * s - 2 + wt
            if not (0 <= kt < 32):
                continue
            for ki in range(2):
                a = 2 * kt + ki
                for qi in range(8):
                    r = 8 * s + qi
                    rs = min(max(r - 4, 0), 56)
                    if not (rs <= a < rs + 8):
                        continue
                    relr = a - r + 7
                    vals = na_rpb[:, :, relr, :][:, :, relc_c]
                    vals = np.where(inwin[None, None], vals, np.float32(NEG))
                    out[:, :, var, wt, ki, :, qi, :] = vals.transpose(0, 1, 3, 2)
    return np.ascontiguousarray(out.reshape(L, 4, 3, 8, 128, 512))


_CACHE = {}


def _get_prog(stop_after=None, dump=()):
    key = (stop_after, tuple(dump))
    if key not in _CACHE:
        p = Prog(stop_after, dump)
        p.build()
        _CACHE[key] = p
    return _CACHE[key]


def _in_maps(inp):
    f = lambda a: np.ascontiguousarray(np.asarray(a, dtype=np.float32))
    tb = _tables()
    w_uq = f(inp["w_uq"])
    perm = np.concatenate([np.arange(16, 32), np.arange(0, 16), np.arange(48, 64), np.arange(32, 48)])
    cols = np.concatenate([hd * 192 + 128 + perm for hd in range(8)])
    w_uqs = np.ascontiguousarray(w_uq[:, :, cols])
    conv_w, conv_b = f(inp["conv_w"]), f(inp["conv_b"])
    cp = np.concatenate([conv_w, conv_b[:, None, :]], axis=1)
    convp = np.ascontiguousarray(cp.reshape(DEPTH, 4, 88, 128).transpose(0, 3, 2, 1))
    shared = dict(
        w_mod=f(inp["w_mod"]), b_mod=f(inp["b_mod"]), g_attn=f(inp["g_attn"]), g_ffn=f(inp["g_ffn"]),
        w_in=f(inp["w_in"]), g_q=f(inp["g_q"]), g_kv=f(inp["g_kv"]), w_uq=w_uq, w_uqs=w_uqs, w_ukv=f(inp["w_ukv"]),
        w_fnet=f(inp["w_fnet"]), w_out=f(inp["w_out"]), w_up=f(inp["w_up"]), convp=convp, w_down=f(inp["w_down"]),
        g_final=f(inp["g_final"]), nab=_na_bias(f(inp["na_rpb"])), **tb)
    x, c, ctx, c_ctx = f(inp["x"]), f(inp["c"]), f(inp["ctx"]), f(inp["c_ctx"])
    maps = []
    for b in range(NCORES):
        m = dict(shared)
        m["xin"] = np.ascontiguousarray(np.concatenate([x[b], ctx[b]], axis=0))
        cc = np.zeros((33, D), np.float32)
        cc[0] = c[b]
        cc[32] = c_ctx
        m["ccT"] = np.ascontiguousarray(cc.reshape(33, 16, 128).transpose(2, 1, 0))
        maps.append(m)
    return maps


def kernel(**inputs):
    prog = _get_prog()
    maps = _in_maps(inputs)
    res = run_bass_kernel_spmd(prog.nc, maps, core_ids=list(range(NCORES)))
    return np.stack([np.asarray(res.results[b]["out"], dtype=np.float32) for b in range(NCORES)], axis=0)
```
